# Optimizing a Trainium2 kernel written in Bass

```python
import math
import jax, jax.numpy as jnp
from jax import lax
import numpy as np

D_MODEL = 2048
BATCH = 1
SEQ = 16384
DEPTH = 1

CHUNK = 64
MEM_TOKENS = 256
NORM_EPS = 1e-6

GDN_HEAD_DIM = 128
GDN_WIDTH = D_MODEL // 2
GDN_HEADS = GDN_WIDTH // GDN_HEAD_DIM
GDN_CONV = 4
GDN_COLS = 4 * GDN_WIDTH + 2 * GDN_HEADS

RWKV_HEAD_DIM = 64
RWKV_WIDTH = D_MODEL - GDN_WIDTH
RWKV_HEADS = RWKV_WIDTH // RWKV_HEAD_DIM
RWKV_DECAY_RANK = 64
RWKV_AAA_RANK = 64
RWKV_GATE_RANK = 160
RWKV_GN_EPS = 64e-5
RWKV_COLS = 3 * RWKV_WIDTH + RWKV_DECAY_RANK + RWKV_AAA_RANK + RWKV_GATE_RANK

IN_PROJ_COLS = GDN_COLS + RWKV_COLS

XA_HEADS = 4
XA_HEAD_DIM = D_MODEL // XA_HEADS

D_FF = 4 * D_MODEL

kernel_name = "hybrid_gdn_rwkv7_xattn_block"


def _rmsnorm(x, gain, eps=NORM_EPS):
    xf = x.astype(jnp.float32)
    y = xf * lax.rsqrt(jnp.mean(xf * xf, axis=-1, keepdims=True) + eps)
    return (y * gain.astype(jnp.float32)).astype(x.dtype)


def _l2norm(x, eps=1e-6):
    return x * lax.rsqrt(jnp.sum(x * x, axis=-1, keepdims=True) + eps)


def _causal_depthwise_conv(x, w):
    K, C = w.shape
    return lax.conv_general_dilated(
        x, w[:, None, :].astype(x.dtype), window_strides=(1,),
        padding=((K - 1, 0),), dimension_numbers=("NWC", "WIO", "NWC"),
        feature_group_count=C)


def _token_shift(y):
    return jnp.pad(y, ((0, 0), (1, 0), (0, 0)))[:, :-1]


def _gated_delta_chunked(q, k, v, g, beta):
    B, T, H, Dk = q.shape
    Dv = v.shape[-1]
    NC = T // CHUNK

    def chunks4(t):
        return t.reshape(B, NC, CHUNK, H, t.shape[-1]).transpose(0, 3, 1, 2, 4)

    def chunks3(t):
        return t.reshape(B, NC, CHUNK, H).transpose(0, 3, 1, 2)

    qc, kc, vc = chunks4(q), chunks4(k), chunks4(v)
    bc = chunks3(beta)
    G = jnp.cumsum(chunks3(g), axis=-1)
    idx = jnp.arange(CHUNK)
    causal = idx[:, None] >= idx[None, :]
    strict = idx[:, None] > idx[None, :]
    diff = G[..., :, None] - G[..., None, :]
    gamma = jnp.where(causal, jnp.exp(jnp.where(causal, diff, 0.0)), 0.0)

    kb = kc * bc[..., None]
    vb = vc * bc[..., None]
    M = jnp.where(strict, jnp.einsum("bhnid,bhnjd->bhnij", kb, kc) * gamma, 0.0)
    eye = jnp.eye(CHUNK, dtype=q.dtype)
    rhs = jnp.concatenate([vb, kb * jnp.exp(G)[..., None]], axis=-1)
    sol = lax.linalg.triangular_solve(M + eye, rhs, left_side=True, lower=True,
                                      unit_diagonal=True)
    U, W = sol[..., :Dv], sol[..., Dv:]
    Aqk = jnp.einsum("bhnid,bhnjd->bhnij", qc, kc) * gamma
    q_dec = qc * jnp.exp(G)[..., None]
    G_last = G[..., -1]
    k_tail = kc * jnp.exp(G_last[..., None] - G)[..., None]

    def step(S, xs):
        U_c, W_c, qd, kt, A_c, gl = xs
        v_new = U_c - jnp.einsum("bhik,bhkv->bhiv", W_c, S)
        o = jnp.einsum("bhik,bhkv->bhiv", qd, S) + jnp.einsum("bhij,bhjv->bhiv", A_c, v_new)
        S = S * jnp.exp(gl)[..., None, None] + jnp.einsum("bhik,bhiv->bhkv", kt, v_new)
        return S, o

    xs = tuple(jnp.moveaxis(t, 2, 0) for t in (U, W, q_dec, k_tail, Aqk, G_last))
    S0 = jnp.zeros((B, H, Dk, Dv), q.dtype)
    _, o = lax.scan(step, S0, xs)
    return o.transpose(1, 0, 3, 2, 4).reshape(B, T, H, Dv)


def _gdn_group(y_g, conv_w, A_log, dt_bias, norm_w):
    B, T, _ = y_g.shape
    H, Dh, GW = GDN_HEADS, GDN_HEAD_DIM, GDN_WIDTH
    qkv = jax.nn.silu(_causal_depthwise_conv(y_g[..., :3 * GW], conv_w)).astype(jnp.float32)
    q, k, v = [t.reshape(B, T, H, Dh) for t in jnp.split(qkv, 3, axis=-1)]
    q = _l2norm(q) * (Dh ** -0.5)
    k = _l2norm(k)
    yf = y_g.astype(jnp.float32)
    z = yf[..., 3 * GW:4 * GW].reshape(B, T, H, Dh)
    a_dt = yf[..., 4 * GW:4 * GW + H]
    b = yf[..., 4 * GW + H:4 * GW + 2 * H]
    g = -jnp.exp(A_log.astype(jnp.float32)) * jax.nn.softplus(a_dt + dt_bias.astype(jnp.float32))
    beta = jax.nn.sigmoid(b)
    o = _gated_delta_chunked(q, k, v, g, beta)
    o = o * lax.rsqrt(jnp.mean(o * o, axis=-1, keepdims=True) + NORM_EPS)
    o = o * norm_w.astype(jnp.float32) * jax.nn.silu(z)
    return o.reshape(B, T, GW)


def _rwkv7_scan(r, w, k, v, a_vec, b_vec):
    B, T, H, N = r.shape

    def step(S, xs):
        r_t, w_t, k_t, v_t, a_t, b_t = xs
        Sa = jnp.einsum("bhvk,bhk->bhv", S, a_t)
        S = S * w_t[:, :, None, :] + Sa[..., :, None] * b_t[..., None, :] \
            + v_t[..., :, None] * k_t[..., None, :]
        return S, jnp.einsum("bhvk,bhk->bhv", S, r_t)

    xs = tuple(jnp.moveaxis(t, 1, 0) for t in (r, w, k, v, a_vec, b_vec))
    S0 = jnp.zeros((B, H, N, N), r.dtype)
    _, o = lax.scan(step, S0, xs)
    return jnp.moveaxis(o, 0, 1)


def _rwkv_group(y_r, mu, w0, w2, a0, a2, g2, k_k, k_a, r_k, ln_w, ln_b):
    B, T, _ = y_r.shape
    H, N, W = RWKV_HEADS, RWKV_HEAD_DIM, RWKV_WIDTH
    y = y_r.astype(jnp.float32)
    y = y + (_token_shift(y) - y) * mu.astype(jnp.float32)
    r, k, v = y[..., :W], y[..., W:2 * W], y[..., 2 * W:3 * W]
    o0 = 3 * W
    w_lo = y[..., o0:o0 + RWKV_DECAY_RANK]
    a_lo = y[..., o0 + RWKV_DECAY_RANK:o0 + RWKV_DECAY_RANK + RWKV_AAA_RANK]
    g_lo = y[..., o0 + RWKV_DECAY_RANK + RWKV_AAA_RANK:]
    w_log = -jax.nn.softplus(-(w0 + jnp.tanh(w_lo) @ w2)) - 0.5
    decay = jnp.exp(-jnp.exp(w_log))
    a = jax.nn.sigmoid(a0 + a_lo @ a2)
    g = jax.nn.sigmoid(g_lo) @ g2

    def heads(t):
        return t.reshape(B, T, H, N)

    kk = _l2norm(heads(k * k_k))
    k = k * (1.0 + (a - 1.0) * k_a)
    r_h, k_h, v_h, a_h = heads(r), heads(k), heads(v), heads(a)
    o = _rwkv7_scan(r_h, heads(decay), k_h, v_h, -kk, kk * a_h)
    mean = jnp.mean(o, axis=-1, keepdims=True)
    var = jnp.mean(jnp.square(o - mean), axis=-1, keepdims=True)
    o = ((o - mean) * lax.rsqrt(var + RWKV_GN_EPS)).reshape(B, T, W) * ln_w + ln_b
    bonus = jnp.sum(r_h * k_h * r_k, axis=-1, keepdims=True) * v_h
    return (o + bonus.reshape(B, T, W)) * g


def _cross_attention(hn, mn, wq, wk, wv, wo):
    B, T, D = hn.shape
    M = mn.shape[1]
    q = (hn @ wq).reshape(B, T, XA_HEADS, XA_HEAD_DIM)
    k = (mn @ wk).reshape(B, M, XA_HEADS, XA_HEAD_DIM)
    v = (mn @ wv).reshape(B, M, XA_HEADS, XA_HEAD_DIM)
    s = jnp.einsum("bthd,bmhd->bhtm", q, k).astype(jnp.float32) * (XA_HEAD_DIM ** -0.5)
    p = jax.nn.softmax(s, axis=-1).astype(hn.dtype)
    o = jnp.einsum("bhtm,bmhd->bthd", p, v).reshape(B, T, D)
    return o @ wo


def setup_inputs(seed: int = 0) -> dict:
    key = jax.random.key(seed)
    keys = jax.random.split(key, 32)
    counter = [0]

    def nk():
        counter[0] += 1
        return keys[counter[0] - 1]

    f32 = jnp.float32
    L, D = DEPTH, D_MODEL

    def nrm(shape, scale):
        return jax.random.normal(nk(), shape, f32) * scale

    def gain(shape):
        return 1.0 + nrm(shape, 0.02)

    x = nrm((BATCH, SEQ, D), 1.0)
    mem = nrm((BATCH, MEM_TOKENS, D), 1.0)
    norm_mix = gain((L, D))
    w_in = nrm((L, D, IN_PROJ_COLS), D ** -0.5)
    gdn_conv_w = nrm((L, GDN_CONV, 3 * GDN_WIDTH), GDN_CONV ** -0.5)
    gdn_A_log = jnp.log(jax.random.uniform(nk(), (L, GDN_HEADS), f32, 1.0, 16.0))
    dt = jnp.exp(jax.random.uniform(nk(), (L, GDN_HEADS), f32, math.log(1e-3), math.log(1e-1)))
    gdn_dt_bias = dt + jnp.log(-jnp.expm1(-dt))
    gdn_norm_w = gain((L, GDN_HEAD_DIM))
    rwkv_mu = jax.random.uniform(nk(), (L, RWKV_COLS), f32, 0.0, 1.0)
    rwkv_w0 = jax.random.uniform(nk(), (L, RWKV_WIDTH), f32, -6.0, -0.5)
    rwkv_w2 = nrm((L, RWKV_DECAY_RANK, RWKV_WIDTH), 0.2 * RWKV_DECAY_RANK ** -0.5)
    rwkv_a0 = nrm((L, RWKV_WIDTH), 0.1)
    rwkv_a2 = nrm((L, RWKV_AAA_RANK, RWKV_WIDTH), 0.5 * RWKV_AAA_RANK ** -0.5)
    rwkv_g2 = nrm((L, RWKV_GATE_RANK, RWKV_WIDTH), RWKV_GATE_RANK ** -0.5)
    rwkv_k_k = 0.85 + nrm((L, RWKV_WIDTH), 0.02)
    rwkv_k_a = gain((L, RWKV_WIDTH))
    rwkv_r_k = nrm((L, RWKV_HEADS, RWKV_HEAD_DIM), 0.1)
    rwkv_ln_w = gain((L, RWKV_WIDTH))
    rwkv_ln_b = nrm((L, RWKV_WIDTH), 0.01)
    w_out = nrm((L, D, D), D ** -0.5)
    norm_xattn = gain((L, D))
    norm_mem = gain((L, D))
    xattn_wq = nrm((L, D, D), D ** -0.5)
    xattn_wk = nrm((L, D, D), D ** -0.5)
    xattn_wv = nrm((L, D, D), D ** -0.5)
    xattn_wo = nrm((L, D, D), D ** -0.5)
    norm_mlp = gain((L, D))
    mlp_w_up = nrm((L, D, D_FF), D ** -0.5)
    mlp_w_down = nrm((L, D_FF, D), D_FF ** -0.5)
    norm_final = gain((D,))
    return {"x": x, "mem": mem, "norm_mix": norm_mix, "w_in": w_in,
            "gdn_conv_w": gdn_conv_w, "gdn_A_log": gdn_A_log, "gdn_dt_bias": gdn_dt_bias,
            "gdn_norm_w": gdn_norm_w, "rwkv_mu": rwkv_mu, "rwkv_w0": rwkv_w0,
            "rwkv_w2": rwkv_w2, "rwkv_a0": rwkv_a0, "rwkv_a2": rwkv_a2, "rwkv_g2": rwkv_g2,
            "rwkv_k_k": rwkv_k_k, "rwkv_k_a": rwkv_k_a, "rwkv_r_k": rwkv_r_k,
            "rwkv_ln_w": rwkv_ln_w, "rwkv_ln_b": rwkv_ln_b, "w_out": w_out,
            "norm_xattn": norm_xattn, "norm_mem": norm_mem, "xattn_wq": xattn_wq,
            "xattn_wk": xattn_wk, "xattn_wv": xattn_wv, "xattn_wo": xattn_wo,
            "norm_mlp": norm_mlp, "mlp_w_up": mlp_w_up, "mlp_w_down": mlp_w_down,
            "norm_final": norm_final}


def reference(x, mem, norm_mix, w_in, gdn_conv_w, gdn_A_log, gdn_dt_bias, gdn_norm_w,
              rwkv_mu, rwkv_w0, rwkv_w2, rwkv_a0, rwkv_a2, rwkv_g2, rwkv_k_k, rwkv_k_a,
              rwkv_r_k, rwkv_ln_w, rwkv_ln_b, w_out, norm_xattn, norm_mem, xattn_wq,
              xattn_wk, xattn_wv, xattn_wo, norm_mlp, mlp_w_up, mlp_w_down, norm_final):
    h = x
    for l in range(DEPTH):
        xn = _rmsnorm(h, norm_mix[l])
        y = xn @ w_in[l]
        o_gdn = _gdn_group(y[..., :GDN_COLS], gdn_conv_w[l], gdn_A_log[l],
                           gdn_dt_bias[l], gdn_norm_w[l])
        o_rwkv = _rwkv_group(y[..., GDN_COLS:], rwkv_mu[l], rwkv_w0[l], rwkv_w2[l],
                             rwkv_a0[l], rwkv_a2[l], rwkv_g2[l], rwkv_k_k[l], rwkv_k_a[l],
                             rwkv_r_k[l], rwkv_ln_w[l], rwkv_ln_b[l])
        mixed = jnp.concatenate([o_gdn, o_rwkv], axis=-1).astype(h.dtype)
        h = h + mixed @ w_out[l]
        h = h + _cross_attention(_rmsnorm(h, norm_xattn[l]), _rmsnorm(mem, norm_mem[l]),
                                 xattn_wq[l], xattn_wk[l], xattn_wv[l], xattn_wo[l])
        hn = _rmsnorm(h, norm_mlp[l])
        h = h + jnp.square(jax.nn.relu(hn @ mlp_w_up[l])) @ mlp_w_down[l]
    return _rmsnorm(h, norm_final)
```

```python
import numpy as np
import ml_dtypes
from contextlib import ExitStack
import concourse.bass as bass
import concourse.mybir as mybir
from concourse.bass_utils import run_bass_kernel_spmd

F32 = mybir.dt.float32
BF16 = mybir.dt.bfloat16
AF = mybir.ActivationFunctionType
ALU = mybir.AluOpType
AX = mybir.AxisListType


class Buf:
    __slots__ = ("w", "r")

    def __init__(self):
        self.w = None
        self.r = []


class Tile:
    def __init__(self, h):
        self.h = h
        self.b = Buf()

    def __getitem__(self, k):
        return self.h[k]


class Sched:
    LIMIT = 16000
    NDMA = 12

    def __init__(self, nc, es):
        self.nc, self.es = nc, es
        self.E = {"pe": nc.tensor, "act": nc.scalar, "dve": nc.vector,
                  "pool": nc.gpsimd, "sp": nc.sync}
        self.cur = {}
        self.seen = {e: {} for e in self.E}
        self.nsem = 0
        self.dq = {}
        self.dqi = {}
        self.nops = {e: 0 for e in self.E}

    def newsem(self):
        self.nsem += 1
        return self.es.enter_context(self.nc.semaphore(f"s{self.nsem}"))

    def _wait(self, e, tok):
        sem, val, te = tok
        if te == e and e == "pe":
            return
        k = id(sem)
        if self.seen[e].get(k, 0) >= val:
            return
        self.seen[e][k] = val
        self.E[e].wait_ge(sem, val)

    def _deps(self, e, reads, writes):
        for b in reads:
            b = b.b if isinstance(b, Tile) else b
            if b.w is not None:
                self._wait(e, b.w)
        for b in writes:
            b = b.b if isinstance(b, Tile) else b
            if b.w is not None:
                self._wait(e, b.w)
            for t in b.r:
                self._wait(e, t)

    def _commit(self, tok, reads, writes):
        for b in writes:
            b = b.b if isinstance(b, Tile) else b
            b.w = tok
            b.r = []
        for b in reads:
            b = b.b if isinstance(b, Tile) else b
            if b.w is tok:
                continue
            b.r = [t for t in b.r if t[0] is not tok[0]] + [tok]

    def op(self, e, fn, reads=(), writes=()):
        self._deps(e, reads, writes)
        c = self.cur.get(e)
        if c is None or c[1] >= self.LIMIT:
            c = [self.newsem(), 0]
            self.cur[e] = c
        c[1] += 1
        ins = fn(self.E[e])
        ins.then_inc(c[0], 1)
        tok = (c[0], c[1], e)
        self._commit(tok, reads, writes)
        self.nops[e] += 1
        return tok

    def dma(self, q, out, in_, reads=(), writes=(), **kw):
        self._deps(q, reads, writes)
        if q not in self.dq:
            self.dq[q] = [[self.newsem(), 0] for _ in range(self.NDMA)]
            self.dqi[q] = 0
        ring = self.dq[q]
        i = self.dqi[q]
        self.dqi[q] = (i + 1) % self.NDMA
        s = ring[i]
        if s[1] >= 48000:
            s = [self.newsem(), 0]
            ring[i] = s
        if s[1] > 0:
            self._wait(q, (s[0], s[1], "dma"))
        s[1] += 16
        self.E[q].dma_start(out=out, in_=in_, **kw).then_inc(s[0], 16)
        tok = (s[0], s[1], "dma")
        self._commit(tok, reads, writes)
        self.nops[q] += 1
        return tok


    def barrier(self):
        toks = [(c[0], c[1], e) for e, c in self.cur.items()]
        for q, ring in self.dq.items():
            toks += [(s[0], s[1], "dma") for s in ring if s[1] > 0]
        for e in self.E:
            for t in toks:
                self._wait(e, t)

    def wait_tok(self, e, tok):
        self._wait(e, tok)


class Ctx:
    pass


def ops(S):
    o = Ctx()
    o.tt = lambda e, out, a, b, op, R, W: S.op(e, lambda g: g.tensor_tensor(out, a, b, op), reads=R, writes=W)
    o.ts = lambda e, out, a, s1, s2, op0, op1, R, W: S.op(
        e, (lambda g: g.tensor_scalar(out, a, s1, s2, op0, op1)) if op1 is not None else (lambda g: g.tensor_scalar(out, a, s1, None, op0)),
        reads=R, writes=W)
    o.stt = lambda e, out, a, sc, b, op0, op1, R, W: S.op("dve", lambda g: g.scalar_tensor_tensor(out, a, sc, b, op0, op1), reads=R, writes=W)
    o.act = lambda out, in_, func, R, W, bias=0.0, scale=1.0: S.op(
        "act", lambda g: g.activation(out, in_, func, bias=bias, scale=scale), reads=R, writes=W)
    o.cp = lambda e, out, in_, R, W: (o.act(out, in_, AF.Copy, R, W) if e == "act" else S.op(e, lambda g: g.tensor_copy(out, in_), reads=R, writes=W))
    o.mm = lambda out, lhsT, rhs, st, sp, R, W: S.op("pe", lambda g: g.matmul(out, lhsT, rhs, start=st, stop=sp), reads=R, writes=W)
    o.tr = lambda out, in_, ident, R, W: S.op("pe", lambda g: g.transpose(out, in_, ident), reads=R, writes=W)
    return o


class Banks:
    def __init__(self, tiles):
        self.t = tiles
        self.i = 0

    def get(self):
        t = self.t[self.i]
        self.i = (self.i + 1) % len(self.t)
        return t


def phase_b(nc, S, es, D, NT, PS, PSB, cst):
    O = ops(S)
    TG = min(512, NT)
    NG = NT // TG
    HW = min(512, TG)
    NH = TG // HW
    KC = 16
    eps = 1e-6

    def sb(name, shape, dt):
        return Tile(es.enter_context(nc.sbuf_tensor("sb_" + name, shape, dt)))

    h = sb("h", [128, KC, TG], F32)
    hn = sb("hn", [128, KC, TG], BF16)
    wk_ = sb("wk_", [128, KC, TG], BF16)
    gains = cst["gains"]
    onesb = cst["onesb"]
    identb = cst["identb"]
    sel = cst["sel"]
    NWB = 3
    wst = [sb(f"wst{i}", [128, KC * 128], F32) for i in range(NWB)]
    wbf = [sb(f"wbf{i}", [128, KC, 128], BF16) for i in range(NWB)]
    wctr = [0]
    KT = sb("KT", [128, KC, 256], BF16)
    V = sb("V", [128, 2, 2048], BF16)
    rstd = sb("rstd", [128, HW], F32)
    sqt = [sb(f"sqt{i}", [128, HW], BF16) for i in range(2)]
    relu_t = [sb(f"relu{i}", [128, HW], BF16) for i in range(2)]
    pT = sb("pT", [128, 2, HW], BF16)
    sm_p = [sb(f"smp{i}", [128, 256], BF16) for i in range(2)]
    sm_f = [sb(f"smf{i}", [128, 256], F32) for i in range(2)]
    sm_s = [sb(f"sms{i}", [128, 4], F32) for i in range(2)]
    selt = [sb(f"selt{i}", [128, 8, HW], BF16) for i in range(2)]
    cnt = {"e": 0}
    ntmp = [sb(f"ntmp{i}", [128, HW], F32) for i in range(2)]
    isel = sb("isel", [128, 8, 128], BF16)
    for s_ in range(8):
        O.ts("dve", isel[:, s_, :], cst["identf"][:, :], sel[:, s_:s_ + 1], None, ALU.mult, None, [cst["identf"], sel], [isel])

    def load_panel(src2d):
        i = wctr[0] % NWB
        wctr[0] += 1
        S.dma("sp", wst[i][:, :], src2d, writes=[wst[i]])
        eng = "pool" if (wctr[0] % 3) else "act"
        O.cp(eng, wbf[i][:, :, :], wst[i][:, :].rearrange("p (k m) -> p k m", k=KC), [wst[i]], [wbf[i]])
        return wbf[i]

    def rmsnorm_to(dst, src, gi, width, nhalves, hw):
        for hh in range(nhalves):
            cs = slice(hh * hw, (hh + 1) * hw)
            pb = PS.get()
            for kc in range(KC):
                sq = sqt[kc % 2]
                O.act(sq[:, :hw], src[:, kc, cs], AF.Square, [src], [sq])
                O.mm(pb[:, :hw], onesb[:, :], sq[:, :hw], kc == 0, kc == KC - 1, [sq, onesb], [pb])
            O.act(rstd[:, :hw], pb[:, :hw], AF.Sqrt, [pb], [rstd], bias=eps, scale=1.0 / 2048.0)
            S.op("dve", lambda g_: g_.reciprocal(rstd[:, :hw], rstd[:, :hw]), reads=[rstd], writes=[rstd])
            for kc in range(KC):
                tm = ntmp[kc % 2]
                O.tt("pool", tm[:, :hw], src[:, kc, cs], rstd[:, :hw], ALU.mult, [src, rstd], [tm])
                O.ts("dve", dst[:, kc, cs], tm[:, :hw], gains[:, kc, gi:gi + 1], None, ALU.mult, None, [tm, gains], [dst])

    def proj(wsrc_fn, nm, src, evac_fn, ncols):
        nh = ncols // min(512, ncols)
        hw = min(512, ncols)
        for m in range(nm):
            wt = load_panel(wsrc_fn(m))
            for hh in range(nh):
                pb = PS.get()
                for kc in range(KC):
                    O.mm(pb[:, :hw], wt[:, kc, :], src[:, kc, hh * hw:(hh + 1) * hw], kc == 0, kc == KC - 1, [wt, src], [pb])
                evac_fn(m, hh, hw, pb)

    memf = sb("memf", [128, KC, 256], F32)
    S.dma("sp", memf[:, :, :], D["memT"].rearrange("(k p) t -> p k t", p=128), writes=[memf])
    mn = sb("mn", [128, KC, 256], BF16)
    rmsnorm_to(mn, memf, 2, 256, 1, 256)

    def ev_k(m, hh, hw, pb):
        O.cp("act", KT[:, m, :], pb[:, :256], [pb], [KT])
    proj(lambda m: D["wk"][m], 16, mn, ev_k, 256)
    VT = sb("VT", [128, KC, 256], BF16)

    def ev_v(m, hh, hw, pb):
        O.cp("act", VT[:, m, :], pb[:, :256], [pb], [VT])
    proj(lambda m: D["wv"][m], 16, mn, ev_v, 256)
    for m in range(16):
        pbt = PSB.get()
        for mt in range(2):
            O.tr(pbt[:, mt * 128:(mt + 1) * 128], VT[:, m, mt * 128:(mt + 1) * 128], identb[:, :], [VT, identb], [pbt])
        O.cp("dve", V[:, :, m * 128:(m + 1) * 128], pbt[:, 0:256].rearrange("p (a b) -> p a b", a=2), [pbt], [V])

    out_toks = []
    for g in range(NG):
        t0 = g * TG
        S.dma("sp", h[:, :, :], D["xres"][:, t0:t0 + TG].rearrange("(k p) t -> p k t", p=128), writes=[h])
        if "mixT" in D:
            S.dma("sp", hn[:, :, :], D["mixT"][:, t0:t0 + TG].rearrange("(k p) t -> p k t", p=128), writes=[hn])
        for kc in (range(KC) if "mixT" not in D else []):
            for hh in range(NH):
                st = selt[cnt["e"] % 2]
                cnt["e"] += 1
                cs = slice(hh * HW, (hh + 1) * HW)
                S.dma("sp", st[:, :, :], D["agout"][kc * 128:(kc + 1) * 128, :].rearrange("p (s t) -> p s t", s=8)[:, :, t0 + hh * HW:t0 + (hh + 1) * HW],
                      reads=[D["agout_b"]], writes=[st])
                pb = PS.get()
                for s_ in range(8):
                    O.mm(pb[:, :HW], isel[:, s_, :], st[:, s_, :], s_ == 0, s_ == 7, [isel, st], [pb])
                O.cp("act" if (cnt["e"] % 2) else "dve", hn[:, kc, cs], pb[:, :HW], [pb], [hn])

        def ev_res(m, hh, hw, pb):
            cs = slice(hh * hw, (hh + 1) * hw)
            O.tt("dve", h[:, m, cs], h[:, m, cs], pb[:, :hw], ALU.add, [h, pb], [h])
        proj(lambda m: D["wout"][m], 16, hn, ev_res, TG)

        rmsnorm_to(hn, h, 1, TG, NH, HW)

        def ev_q(m, hh, hw, pb):
            cs = slice(hh * hw, (hh + 1) * hw)
            O.cp("act", wk_[:, m, cs], pb[:, :hw], [pb], [wk_])
        proj(lambda m: D["wq"][m], 16, hn, ev_q, TG)
        scale = 512.0 ** -0.5
        for hd in range(4):
            for hh in range(NH):
                ntt = HW // 128
                for tt_ in range(ntt):
                    tok = slice(hh * HW + tt_ * 128, hh * HW + (tt_ + 1) * 128)
                    pb = PS.get()
                    for j in range(4):
                        O.mm(pb[:, :256], wk_[:, hd * 4 + j, tok], KT[:, hd * 4 + j, :], j == 0, j == 3, [wk_, KT], [pb])
                    i2 = (tt_ + hh) % 2
                    ss = sm_s[i2]
                    O_red = S.op("dve", lambda g_, pb=pb, ss=ss: g_.tensor_reduce(ss[:, 0:1], pb[:, :256], AX.X, ALU.max), reads=[pb], writes=[ss])
                    O.ts("dve", ss[:, 1:2], ss[:, 0:1], -scale, None, ALU.mult, None, [ss], [ss])
                    S.op("act", lambda g_, pb=pb, ss=ss, i2=i2: g_.activation(sm_f[i2][:, :], pb[:, :256], AF.Exp, bias=ss[:, 1:2], scale=scale, accum_out=ss[:, 2:3]),
                         reads=[pb, ss], writes=[sm_f[i2], ss])
                    S.op("dve", lambda g_, ss=ss: g_.reciprocal(ss[:, 3:4], ss[:, 2:3]), reads=[ss], writes=[ss])
                    O.ts("pool", sm_p[i2][:, :], sm_f[i2][:, :], ss[:, 3:4], None, ALU.mult, None, [sm_f[i2], ss], [sm_p[i2]])
                    pbt = PSB.get()
                    for mt in range(2):
                        O.tr(pbt[:, mt * 128:(mt + 1) * 128], sm_p[i2][:, mt * 128:(mt + 1) * 128], identb[:, :], [sm_p[i2], identb], [pbt])
                    O.cp("act", pT[:, :, tt_ * 128:(tt_ + 1) * 128], pbt[:, 0:256].rearrange("p (a b) -> p a b", a=2), [pbt], [pT])
                for j in range(4):
                    pb = PS.get()
                    for mt in range(2):
                        O.mm(pb[:, :HW], V[:, mt, (hd * 4 + j) * 128:(hd * 4 + j + 1) * 128], pT[:, mt, :], mt == 0, mt == 1, [V, pT], [pb])
                    O.cp("dve", hn[:, hd * 4 + j, hh * HW:(hh + 1) * HW], pb[:, :HW], [pb], [hn])
        proj(lambda m: D["wo"][m], 16, hn, ev_res, TG)

        rmsnorm_to(hn, h, 3, TG, NH, HW)
        for kq in range(4):
            def ev_up(m, hh, hw, pb):
                cs = slice(hh * hw, (hh + 1) * hw)
                rt = relu_t[(m + hh) % 2]
                O.act(rt[:, :hw], pb[:, :hw], AF.Relu, [pb], [rt])
                O.tt("pool", wk_[:, m, cs], rt[:, :hw], rt[:, :hw], ALU.mult, [rt], [wk_])
            proj(lambda m: D["wup"][kq * 16 + m], 16, hn, ev_up, TG)
            proj(lambda m: D["wdn"][kq, m], 16, wk_, ev_res, TG)

        for hh in range(NH):
            cs = slice(hh * HW, (hh + 1) * HW)
            pb = PS.get()
            for kc in range(KC):
                sq = sqt[kc % 2]
                O.act(sq[:, :HW], h[:, kc, cs], AF.Square, [h], [sq])
                O.mm(pb[:, :HW], onesb[:, :], sq[:, :HW], kc == 0, kc == KC - 1, [sq, onesb], [pb])
            O.act(rstd[:, :HW], pb[:, :HW], AF.Sqrt, [pb], [rstd], bias=eps, scale=1.0 / 2048.0)
            S.op("dve", lambda g_: g_.reciprocal(rstd[:, :HW], rstd[:, :HW]), reads=[rstd], writes=[rstd])
            for kc in range(KC):
                tm = ntmp[kc % 2]
                O.tt("pool", tm[:, :HW], h[:, kc, cs], rstd[:, :HW], ALU.mult, [h, rstd], [tm])
                O.ts("dve", h[:, kc, cs], tm[:, :HW], gains[:, kc, 4:5], None, ALU.mult, None, [tm, gains], [h])
        out_toks.append(S.dma("sp", D["outT"][:, t0:t0 + TG].rearrange("(k p) t -> p k t", p=128), h[:, :, :], reads=[h]))
    return out_toks


def phase_a(nc, S, es, D, SEQ, PS, PSB, cst, odst):
    O = ops(S)
    BL = 256
    NCH = BL // 64
    NB = SEQ // BL
    KC = 16
    NCOL = 1186
    eps = 1e-6

    def sb(name, shape, dt):
        return Tile(es.enter_context(nc.sbuf_tensor("sa_" + name, shape, dt)))

    identb, identf, onesb = cst["identb"], cst["identf"], cst["onesb"]
    gains = cst["gains"]
    pp = sb("pp", [128, 32], F32)
    S.dma("sp", pp[:, :], D["pp"], writes=[pp])
    bonesf = sb("bonesf", [128, 128], F32)
    S.dma("sp", bonesf[:, :], D["bones"], writes=[bonesf])
    bonesb = sb("bonesb", [128, 128], BF16)
    O.cp("dve", bonesb[:, :], bonesf[:, :], [bonesf], [bonesb])
    onesf = sb("onesf", [128, 128], F32)
    S.op("pool", lambda g: g.memset(onesf[:, :], 1.0), writes=[onesf])
    esel = sb("esel", [34, 2, 128], F32)
    S.dma("sp", esel[:, :, :], D["esel"], writes=[esel])
    masks = sb("masks", [64, 3, 64], F32)
    S.dma("sp", masks[:, :, :], D["masks"], writes=[masks])
    NM = sb("NM", [64, NCH, 64], F32)
    SM = sb("SM", [64, NCH, 64], F32)
    CM = sb("CM", [64, NCH, 2, 64], F32)
    Ibc = sb("Ibc", [64, NCH, 64], BF16)
    for c in range(NCH):
        O.cp("pool", NM[:, c, :], masks[:, 0, :], [masks], [NM])
        O.cp("pool", SM[:, c, :], masks[:, 1, :], [masks], [SM])
        O.cp("pool", CM[:, c, :, :], masks[:, 1:3, :], [masks], [CM])
        O.cp("dve", Ibc[:, c, :], identf[0:64, 0:64], [identf], [Ibc])
    O.act(pp[:, 28:29], pp[:, 25:26], AF.Exp, [pp], [pp])
    O.ts("dve", pp[:, 28:29], pp[:, 28:29], -1.0, None, ALU.mult, None, [pp], [pp])
    O.ts("dve", pp[:, 29:30], pp[:, 21:22], -1.0, 1.0, ALU.mult, ALU.add, [pp], [pp])
    wa2f = sb("wa2f", [128, 128], F32)
    S.dma("sp", wa2f[:, :], D["wa2"], writes=[wa2f])
    wa2b = sb("wa2b", [128, 128], BF16)
    O.cp("dve", wa2b[:, :], wa2f[:, :], [wa2f], [wa2b])
    g2f = sb("g2f", [128, 2, 128], F32)
    S.dma("sp", g2f[:, :, :], D["g2"], writes=[g2f])
    g2b = sb("g2b", [128, 2, 128], BF16)
    O.cp("dve", g2b[:, :, :], g2f[:, :, :], [g2f], [g2b])
    wres = sb("wres", [128, KC, NCOL], BF16)
    wtmp = [sb(f"wtmp{i}", [128, NCOL], F32) for i in range(2)]
    for kc in range(KC):
        wt = wtmp[kc % 2]
        S.dma("sp", wt[:, :], D["win"][:, kc, :], writes=[wt])
        O.cp("act" if kc % 2 else "pool", wres[:, kc, :], wt[:, :], [wt], [wres])

    xs = sb("xs", [128, KC, BL], F32)
    xb = sb("xb", [128, KC, BL], BF16)
    sqt = [sb(f"sq{i}", [128, BL], BF16) for i in range(2)]
    rstd = sb("rstd", [128, BL], F32)
    Y = [sb(f"y{i}", [128, 4 + BL], F32) for i in range(10)]
    for yt in Y:
        S.op("pool", lambda g, yt=yt: g.memset(yt[:, :], 0.0), writes=[yt])
    NF = 30
    F = [sb(f"f{i}", [128, BL], F32) for i in range(NF)]
    NBF = 12
    Bq = [sb(f"b{i}", [128, BL], BF16) for i in range(NBF)]
    XA = [sb(f"xa{i}", [128, NCH, 2, 64], BF16) for i in range(2)]
    GMg = sb("gmg", [64, NCH, 2, 64], F32)
    PCt = [sb(f"pc{i}", [128, NCH], F32) for i in range(2)]
    TM = [[sb(f"tm{g}_{i}", [64, NCH, 128], BF16) for i in range(4)] for g in range(2)]
    AM = [sb(f"am{i}", [64, NCH, 4, 64], BF16) for i in range(3)]
    XX = [[sb(f"xx{hh}_{i}", [64, NCH, 2, 64], BF16) for i in range(2)] for hh in range(3)]
    YY = [[sb(f"yy{hh}_{i}", [64, NCH, 64], BF16) for i in range(2)] for hh in range(3)]
    TA = [sb(f"ta{i}", [128, NCH, 64], BF16) for i in range(2)]
    AV = [sb(f"av{i}", [64, NCH, 128], BF16) for i in range(3)]
    U0 = [sb(f"u0{i}", [64, NCH, 128], F32) for i in range(3)]
    Ub = [sb(f"ub{i}", [64, 128], BF16) for i in range(3)]
    O32 = [sb(f"o32{i}", [64, NCH, 128], F32) for i in range(2)]
    H32 = [sb(f"h32{i}", [128, 128], F32) for i in range(2)]
    Hbf = [sb(f"hbf{i}", [128, 128], BF16) for i in range(2)]
    for t in H32 + Hbf:
        S.op("pool", lambda g, t=t: g.memset(t[:, :], 0.0), writes=[t])
    ob = [sb(f"ob{i}", [128, BL], BF16) for i in range(2)]

    v3 = lambda ap: ap.rearrange("p (c t) -> p c t", c=NCH)

    def rsqrt_ps(dst, pb, scale, bias):
        O.act(dst, pb, AF.Sqrt, [pb], [dst_t[0]], bias=bias, scale=scale)

    def cumsum(src, A, Bt):
        cur = src
        bufs = [A, Bt]
        for i, d in enumerate([1, 2, 4, 8, 16, 32]):
            dst = bufs[i % 2]
            O.cp("pool", v3(dst[:, :])[:, :, 0:d], v3(cur[:, :])[:, :, 0:d], [cur], [dst])
            O.tt("dve", v3(dst[:, :])[:, :, d:], v3(cur[:, :])[:, :, d:], v3(cur[:, :])[:, :, :64 - d], ALU.add, [cur], [dst])
            cur = dst
        return cur

    def decay_terms(L, dEnd, eEnd, PC):
        for c in range(NCH):
            O.ts("pool", dEnd[:, c * 64:(c + 1) * 64], L[:, c * 64:(c + 1) * 64], -1.0, L[:, c * 64 + 63:c * 64 + 64], ALU.mult, ALU.add, [L], [dEnd])
        O.act(eEnd[:, :], dEnd[:, :], AF.Exp, [dEnd], [eEnd])
        O.act(PC[:, :], v3(L[:, :])[:, :, 63], AF.Exp, [L], [PC])

    for b in range(NB):
        t0 = b * BL
        S.dma("sp", xs[:, :, :], D["xT"][:, t0:t0 + BL].rearrange("(k p) t -> p k t", p=128), writes=[xs])
        pbs = PS.get()
        for kc in range(KC):
            O.ts("pool", xb[:, kc, :], xs[:, kc, :], gains[:, kc, 0:1], None, ALU.mult, None, [xs, gains], [xb])
            sq = sqt[kc % 2]
            O.act(sq[:, :], xs[:, kc, :], AF.Square, [xs], [sq])
            O.mm(pbs[:, :BL], onesb[:, :], sq[:, :], kc == 0, kc == KC - 1, [sq, onesb], [pbs])
        O.act(rstd[:, :], pbs[:, :BL], AF.Sqrt, [pbs], [rstd], bias=eps, scale=1.0 / 2048.0)
        S.op("dve", lambda g: g.reciprocal(rstd[:, :], rstd[:, :]), reads=[rstd], writes=[rstd])
        for ct in range(10):
            M = 128 if ct < 9 else 34
            c0 = ct * 128
            pb = PS.get()
            for kc in range(KC):
                O.mm(pb[:M, :BL], wres[:, kc, c0:c0 + M], xb[:, kc, :], kc == 0, kc == KC - 1, [wres, xb], [pb])
            yt = Y[ct]
            if b > 0:
                O.cp("pool", yt[:M, 0:4], yt[:M, BL:BL + 4], [yt], [yt])
            O.tt("dve", yt[:M, 4:4 + BL], pb[:M, :BL], rstd[:M, :], ALU.mult, [pb, rstd], [yt])

        cur = lambda ct, rows=128: Y[ct][:rows, 4:4 + BL]
        prv = lambda ct, rows=128, k=1: Y[ct][:rows, 4 - k:4 - k + BL]

        def lerp(dst, ct, mucol, rows=128):
            O.tt("pool", F[29][:rows, :], prv(ct, rows), cur(ct, rows), ALU.subtract, [Y[ct]], [F[29]])
            O.stt("dve", dst[:rows, :], F[29][:rows, :], pp[:rows, mucol:mucol + 1], cur(ct, rows), ALU.mult, ALU.add, [F[29], pp, Y[ct]], [dst])
        rl, kl, vl, wal, gl0, gl1 = F[0], F[1], F[2], F[3], F[4], F[5]
        lerp(rl, 4, 12); lerp(kl, 5, 13); lerp(vl, 6, 14); lerp(wal, 7, 15); lerp(gl0, 8, 16); lerp(gl1, 9, 17, 32)
        tw = Bq[0]
        O.act(tw[0:64, :], wal[0:64, :], AF.Tanh, [wal], [tw])
        O.cp("pool", tw[64:128, :], wal[64:128, :], [wal], [tw])
        pbw = PS.get()
        O.mm(pbw[:, :BL], wa2b[0:64, :], tw[0:64, :], True, True, [wa2b, tw], [pbw])
        logw = F[6]
        O.act(logw[:, :], pbw[:, :BL], AF.Sigmoid, [pbw, pp], [logw], bias=pp[:, 18:19])
        O.ts("pool", logw[:, :], logw[:, :], -0.6065306597126334, None, ALU.mult, None, [logw], [logw])
        pba = PS.get()
        O.mm(pba[:, :BL], wa2b[64:128, :], tw[64:128, :], True, True, [wa2b, tw], [pba])
        eta = F[7]
        O.act(eta[:, :], pba[:, :BL], AF.Sigmoid, [pba, pp], [eta], bias=pp[:, 19:20])
        sg0, sg1 = Bq[1], Bq[2]
        O.act(sg0[:, :], gl0[:, :], AF.Sigmoid, [gl0], [sg0])
        O.act(sg1[0:32, :], gl1[0:32, :], AF.Sigmoid, [gl1], [sg1])
        pbg = PS.get()
        O.mm(pbg[:, :BL], g2b[:, 0, :], sg0[:, :], True, False, [g2b, sg0], [pbg])
        O.mm(pbg[:, :BL], g2b[0:32, 1, :], sg1[0:32, :], False, True, [g2b, sg1], [pbg])
        gate = F[8]
        O.cp("act", gate[:, :], pbg[:, :BL], [pbg], [gate])
        kkr = F[9]
        O.ts("pool", kkr[:, :], kl[:, :], pp[:, 20:21], None, ALU.mult, None, [kl, pp], [kkr])
        sqk = Bq[3]
        O.act(sqk[:, :], kkr[:, :], AF.Square, [kkr], [sqk])
        pbk = PS.get()
        O.mm(pbk[:, :BL], bonesb[:, :], sqk[:, :], True, True, [bonesb, sqk], [pbk])
        rk = F[10]
        O.act(rk[:, :], pbk[:, :BL], AF.Sqrt, [pbk], [rk], bias=1e-6)
        S.op("dve", lambda g: g.reciprocal(rk[:, :], rk[:, :]), reads=[rk], writes=[rk])
        kk = F[11]
        O.tt("pool", kk[:, :], kkr[:, :], rk[:, :], ALU.mult, [kkr, rk], [kk])
        t1 = F[12]
        O.ts("dve", t1[:, :], eta[:, :], pp[:, 21:22], pp[:, 29:30], ALU.mult, ALU.add, [eta, pp], [t1])
        kmod = F[13]
        O.tt("pool", kmod[:, :], kl[:, :], t1[:, :], ALU.mult, [kl, t1], [kmod])
        bvec = F[14]
        O.tt("pool", bvec[:, :], kk[:, :], eta[:, :], ALU.mult, [kk, eta], [bvec])
        L = cumsum(logw, F[15], F[16])
        eL, eLm, enL, dEnd, eEnd = F[17], F[18], F[19], F[20], F[21]
        O.act(eL[:, :], L[:, :], AF.Exp, [L], [eL])
        O.tt("pool", eLm[:, :], L[:, :], logw[:, :], ALU.subtract, [L, logw], [eLm])
        O.act(eLm[:, :], eLm[:, :], AF.Exp, [eLm], [eLm])
        O.act(enL[:, :], L[:, :], AF.Exp, [L], [enL], scale=-1.0)
        decay_terms(L, dEnd, eEnd, PCt[1])
        xa = XA[1]
        O.stt("dve", xa[:, :, 0, :], v3(kk[:, :]), -1.0, v3(eLm[:, :]), ALU.mult, ALU.mult, [kk, eLm], [xa])
        O.tt("pool", xa[:, :, 1, :], v3(rl[:, :]), v3(eL[:, :]), ALU.mult, [rl, eL], [xa])
        Ybr, Ykr, BoTr, KoTr, vTr = Bq[4], Bq[5], Bq[6], Bq[7], Bq[8]
        O.tt("dve", Ybr[:, :], bvec[:, :], enL[:, :], ALU.mult, [bvec, enL], [Ybr])
        O.tt("pool", Ykr[:, :], kmod[:, :], enL[:, :], ALU.mult, [kmod, enL], [Ykr])
        O.tt("dve", BoTr[:, :], bvec[:, :], eEnd[:, :], ALU.mult, [bvec, eEnd], [BoTr])
        O.tt("pool", KoTr[:, :], kmod[:, :], eEnd[:, :], ALU.mult, [kmod, eEnd], [KoTr])
        O.cp("act", vTr[:, :], vl[:, :], [vl], [vTr])
        t2 = Bq[9]
        O.tt("pool", F[22][:, :], rl[:, :], kmod[:, :], ALU.mult, [rl, kmod], [F[22]])
        O.ts("dve", t2[:, :], F[22][:, :], pp[:, 22:23], None, ALU.mult, None, [F[22], pp], [t2])
        pbb = PS.get()
        O.mm(pbb[:, :BL], bonesb[:, :], t2[:, :], True, True, [bonesb, t2], [pbb])
        bonus = F[22]
        O.tt("dve", bonus[:, :], pbb[:, :BL], vl[:, :], ALU.mult, [pbb, vl], [bonus])

        def conv(dst, ct, col):
            O.ts("pool", dst[:, :], cur(ct), pp[:, col + 3:col + 4], None, ALU.mult, None, [Y[ct], pp], [dst])
            for j in range(3):
                O.stt("dve", dst[:, :], prv(ct, 128, 3 - j), pp[:, col + j:col + j + 1], dst[:, :], ALU.mult, ALU.add, [Y[ct], pp, dst], [dst])
            O.act(dst[:, :], dst[:, :], AF.Silu, [dst], [dst])
        qs, ks, vs = F[0], F[1], F[2]
        conv(qs, 0, 0); conv(ks, 1, 4); conv(vs, 2, 8)

        def l2n(dst, src, mul, sqb):
            O.act(sqb[:, :], src[:, :], AF.Square, [src], [sqb])
            pb = PS.get()
            O.mm(pb[:, :BL], onesb[:, :], sqb[:, :], True, True, [onesb, sqb], [pb])
            O.act(F[29][:, :], pb[:, :BL], AF.Sqrt, [pb], [F[29]], bias=1e-6)
            S.op("dve", lambda g: g.reciprocal(F[29][:, :], F[29][:, :]), reads=[F[29]], writes=[F[29]])
            O.stt("dve", dst[:, :], src[:, :], mul, F[29][:, :], ALU.mult, ALU.mult, [src, F[29]], [dst])
        qn, kn = F[3], F[4]
        l2n(qn, qs, 128.0 ** -0.5, Bq[0])
        l2n(kn, ks, 1.0, Bq[1])
        sz = F[5]
        O.act(sz[:, :], cur(3), AF.Silu, [Y[3]], [sz])
        pbd = PS.get()
        O.mm(pbd[:, :BL], esel[0:34, 0, :], cur(9, 34), True, True, [esel, Y[9]], [pbd])
        pbe = PS.get()
        O.mm(pbe[:, :BL], esel[0:34, 1, :], cur(9, 34), True, True, [esel, Y[9]], [pbe])
        gg = F[6]
        O.act(gg[:, :], pbd[:, :BL], AF.Exp, [pbd, pp], [gg], bias=pp[:, 26:27])
        O.act(gg[:, :], gg[:, :], AF.Ln, [gg], [gg], bias=1.0)
        O.ts("pool", gg[:, :], gg[:, :], pp[:, 28:29], None, ALU.mult, None, [gg, pp], [gg])
        beta = F[7]
        O.act(beta[:, :], pbe[:, :BL], AF.Sigmoid, [pbe], [beta])
        G = cumsum(gg, F[15], F[16])
        eG, dEndG, eEndG = F[17], F[18], F[19]
        O.act(eG[:, :], G[:, :], AF.Exp, [G], [eG])
        decay_terms(G, dEndG, eEndG, PCt[0])
        kb = F[9]
        O.tt("pool", kb[:, :], kn[:, :], beta[:, :], ALU.mult, [kn, beta], [kb])
        xg = XA[0]
        O.ts("dve", xg[:, :, 0, :], v3(kb[:, :]), -1.0, None, ALU.mult, None, [kb], [xg])
        O.cp("pool", xg[:, :, 1, :], v3(qn[:, :]), [qn], [xg])
        Ybg, Ykg, AiTg, RiTg, BoTg, KoTg, vTg = Bq[2], Bq[3], Bq[10], Bq[11], Bq[0], Bq[1], Bq[9]
        O.cp("act", Ybg[:, :], kn[:, :], [kn], [Ybg])
        O.cp("pool", Ykg[:, :], kb[:, :], [kb], [Ykg])
        O.stt("dve", AiTg[:, :], kb[:, :], -1.0, eG[:, :], ALU.mult, ALU.mult, [kb, eG], [AiTg])
        O.tt("pool", RiTg[:, :], qn[:, :], eG[:, :], ALU.mult, [qn, eG], [RiTg])
        O.tt("dve", BoTg[:, :], kn[:, :], eEndG[:, :], ALU.mult, [kn, eEndG], [BoTg])
        O.tt("pool", KoTg[:, :], kb[:, :], eEndG[:, :], ALU.mult, [kb, eEndG], [KoTg])
        O.cp("act", vTg[:, :], vs[:, :], [vs], [vTg])
        negG = F[10]
        O.ts("pool", negG[0:1, :], G[0:1, :], -1.0, None, ALU.mult, None, [G], [negG])
        pbD = PS.get()
        for c in range(NCH):
            cs = slice(c * 64, (c + 1) * 64)
            O.mm(pbD[0:64, cs], onesf[0:1, 0:64], G[0:1, cs], True, False, [onesf, G], [pbD])
            O.mm(pbD[0:64, cs], negG[0:1, cs], onesf[0:1, 0:64], False, True, [onesf, negG], [pbD])
        tD = F[11]
        O.tt("dve", tD[0:64, :], pbD[0:64, :BL], NM[:, :, :].rearrange("p c t -> p (c t)"), ALU.add, [pbD, NM], [tD])
        O.act(GMg[:, :, 1, :], v3(tD[0:64, :]), AF.Exp, [tD], [GMg])
        O.tt("pool", GMg[:, :, 0, :], GMg[:, :, 1, :], SM[:, :, :], ALU.mult, [GMg, SM], [GMg])

        groups = [
            dict(g=0, xa=XA[0], Yb=Ybg, Yk=Ykg, AiT=lambda c: AiTg[:, c * 64:(c + 1) * 64], AiTt=AiTg,
                 RiT=lambda hs, c: RiTg[hs, c * 64:(c + 1) * 64], RiTt=RiTg, BoT=BoTg, KoT=KoTg, vT=vTg, GM=GMg, PC=PCt[0], heads=[(0, 128, 0)]),
            dict(g=1, xa=XA[1], Yb=Ybr, Yk=Ykr, AiT=lambda c: XA[1][:, c, 0, :], AiTt=XA[1],
                 RiT=lambda hs, c: XA[1][hs, c, 1, :], RiTt=XA[1], BoT=BoTr, KoT=KoTr, vT=vTr, GM=CM, PC=PCt[1], heads=[(0, 64, 1), (64, 64, 2)]),
        ]
        hlist = []
        for gr in groups:
            g = gr["g"]
            Ain, Bout, Kout, Vt = TM[g]
            for (srcf, srct, dst) in [(gr["AiT"], gr["AiTt"], Ain),
                                      (lambda c, t=gr["BoT"]: t[:, c * 64:(c + 1) * 64], gr["BoT"], Bout),
                                      (lambda c, t=gr["KoT"]: t[:, c * 64:(c + 1) * 64], gr["KoT"], Kout),
                                      (lambda c, t=gr["vT"]: t[:, c * 64:(c + 1) * 64], gr["vT"], Vt)]:
                pbt = PSB.get()
                for c in range(NCH):
                    O.tr(pbt[0:64, c * 128:(c + 1) * 128], srcf(c), identb[:, :], [srct, identb], [pbt])
                O.cp("act", dst[:, :, :], pbt[0:64, 0:NCH * 128].rearrange("p (c f) -> p c f", c=NCH), [pbt], [dst])
            for (p0, dk, hi) in gr["heads"]:
                hs = slice(p0, p0 + dk)
                am = AM[hi]
                xa = gr["xa"]
                for c0 in range(0, NCH, 2):
                    pb = PS.get()
                    for j in range(2):
                        c = c0 + j
                        cs = slice(c * 64, (c + 1) * 64)
                        O.mm(pb[0:64, j * 256:j * 256 + 128], gr["Yb"][hs, cs], xa[hs, c, :, :].rearrange("p a t -> p (a t)"), True, True, [gr["Yb"], xa], [pb])
                        O.mm(pb[0:64, j * 256 + 128:j * 256 + 256], gr["Yk"][hs, cs], xa[hs, c, :, :].rearrange("p a t -> p (a t)"), True, True, [gr["Yk"], xa], [pb])
                    pv = pb[0:64, :].rearrange("p (c a t) -> p c a t", c=2, a=4)
                    O.tt("dve", am[:, c0:c0 + 2, 0:2, :], pv[:, :, 0:2, :], gr["GM"][:, c0:c0 + 2, :, :], ALU.mult, [pb, gr["GM"]], [am])
                    O.tt("dve", am[:, c0:c0 + 2, 2:4, :], pv[:, :, 2:4, :], gr["GM"][:, c0:c0 + 2, :, :], ALU.mult, [pb, gr["GM"]], [am])
                xx, yy = XX[hi], YY[hi]
                pbt = PSB.get()
                for c in range(NCH):
                    O.tr(pbt[0:64, c * 64:(c + 1) * 64], am[:, c, 0, :], identb[0:64, 0:64], [am, identb], [pbt])
                O.cp("act", xx[0][:, :, 1, :], pbt[0:64, 0:NCH * 64].rearrange("p (c t) -> p c t", c=NCH), [pbt], [xx[0]])
                O.cp("pool", xx[0][:, :, 0, :], am[:, :, 0, :], [am], [xx[0]])
                O.tt("pool", yy[0][:, :, :], am[:, :, 0, :], Ibc[:, :, :], ALU.add, [am, Ibc], [yy[0]])
                for it in range(5):
                    src, dst = xx[it % 2], xx[(it + 1) % 2]
                    pb = PS.get()
                    for c in range(NCH):
                        O.mm(pb[0:64, c * 128:c * 128 + 64], src[:, c, 1, :], src[:, c, 0, :], True, True, [src], [pb])
                        O.mm(pb[0:64, c * 128 + 64:c * 128 + 128], src[:, c, 0, :], src[:, c, 1, :], True, True, [src], [pb])
                    O.cp("act" if it % 2 else "dve", dst[:, :, :, :], pb[0:64, 0:NCH * 128].rearrange("p (c a t) -> p c a t", c=NCH, a=2), [pb], [dst])
                    pb2 = PS.get()
                    for c in range(NCH):
                        O.mm(pb2[0:64, c * 64:(c + 1) * 64], dst[:, c, 1, :], yy[it % 2][:, c, :], True, True, [dst, yy[it % 2]], [pb2])
                    O.tt("dve", yy[(it + 1) % 2][:, :, :], yy[it % 2][:, :, :], pb2[0:64, 0:NCH * 64].rearrange("p (c t) -> p c t", c=NCH), ALU.add,
                         [yy[it % 2], pb2], [yy[(it + 1) % 2]])
                TT = yy[1]
                pb = PS.get()
                for c in range(NCH):
                    O.mm(pb[hs, c * 64:(c + 1) * 64], Ain[:, c, hs], TT[:, c, :], True, True, [Ain, TT], [pb])
                O.cp("act", TA[g][hs, :, :], pb[hs, 0:NCH * 64].rearrange("p (c t) -> p c t", c=NCH), [pb], [TA[g]])
                cpb = 512 // dk
                for c0 in range(0, NCH, cpb):
                    pb = PS.get()
                    for j in range(min(cpb, NCH - c0)):
                        c = c0 + j
                        O.mm(pb[0:64, j * dk:(j + 1) * dk], am[:, c, 2, :], Vt[:, c, hs], True, True, [am, Vt], [pb])
                    n = min(cpb, NCH - c0)
                    O.cp("act", AV[hi][:, c0:c0 + n, 0:dk], pb[0:64, 0:n * dk].rearrange("p (c f) -> p c f", c=n), [pb], [AV[hi]])
                    pb = PS.get()
                    for j in range(n):
                        c = c0 + j
                        O.mm(pb[0:64, j * dk:(j + 1) * dk], TT[:, c, :], AV[hi][:, c, 0:dk], True, True, [TT, AV[hi]], [pb])
                    O.cp("dve", U0[hi][:, c0:c0 + n, 0:dk], pb[0:64, 0:n * dk].rearrange("p (c f) -> p c f", c=n), [pb], [U0[hi]])
                hlist.append(dict(gr=gr, g=g, hs=hs, dk=dk, hi=hi, am=am, Vt=Vt, Bout=Bout, Kout=Kout))
        for c in range(NCH):
            pbus = []
            for hd in hlist:
                g, hs, dk, hi = hd["g"], hd["hs"], hd["dk"], hd["hi"]
                pbu = PS.get()
                O.mm(pbu[0:64, 0:dk], TA[g][hs, c, :], Hbf[g][hs, 0:dk], True, True, [TA[g], Hbf[g]], [pbu])
                pbus.append(pbu)
            for hd, pbu in zip(hlist, pbus):
                dk, hi = hd["dk"], hd["hi"]
                O.tt("dve", Ub[hi][:, 0:dk], pbu[0:64, 0:dk], U0[hi][:, c, 0:dk], ALU.add, [pbu, U0[hi]], [Ub[hi]])
            pbhs = []
            for hd in hlist:
                gr, g, hs, dk, hi, am, Vt = hd["gr"], hd["g"], hd["hs"], hd["dk"], hd["hi"], hd["am"], hd["Vt"]
                pbo = PS.get()
                if hs.start == 0:
                    O.mm(pbo[0:64, 0:dk], am[:, c, 3, :], Vt[:, c, hs], True, False, [am, Vt], [pbo])
                    O.mm(pbo[0:64, 0:dk], gr["RiT"](hs, c), Hbf[g][hs, 0:dk], False, False, [gr["RiTt"], Hbf[g]], [pbo])
                    O.mm(pbo[0:64, 0:dk], am[:, c, 1, :], Ub[hi][:, 0:dk], False, True, [am, Ub[hi]], [pbo])
                    O.cp("act", O32[g][:, c, hs], pbo[0:64, 0:dk], [pbo], [O32[g]])
                else:
                    pbr = PS.get()
                    O.mm(pbr[0:64, 0:dk], gr["RiT"](hs, c), Hbf[g][hs, 0:dk], True, True, [gr["RiTt"], Hbf[g]], [pbr])
                    O.mm(pbo[0:64, 0:dk], am[:, c, 3, :], Vt[:, c, hs], True, False, [am, Vt], [pbo])
                    O.mm(pbo[0:64, 0:dk], am[:, c, 1, :], Ub[hi][:, 0:dk], False, True, [am, Ub[hi]], [pbo])
                    O.cp("act", O32[g][:, c, hs], pbr[0:64, 0:dk], [pbr], [O32[g]])
                    O.tt("dve", O32[g][:, c, hs], O32[g][:, c, hs], pbo[0:64, 0:dk], ALU.add, [O32[g], pbo], [O32[g]])
                pbh = PS.get()
                O.mm(pbh[hs, 0:dk], hd["Kout"][:, c, hs], Vt[:, c, hs], True, False, [hd["Kout"], Vt], [pbh])
                O.mm(pbh[hs, 0:dk], hd["Bout"][:, c, hs], Ub[hi][:, 0:dk], False, True, [hd["Bout"], Ub[hi]], [pbh])
                pbhs.append(pbh)
            for hd, pbh in zip(hlist, pbhs):
                gr, g, hs, dk = hd["gr"], hd["g"], hd["hs"], hd["dk"]
                O.stt("dve", Hbf[g][hs, 0:dk], H32[g][hs, 0:dk], gr["PC"][hs, c:c + 1], pbh[hs, 0:dk], ALU.mult, ALU.add, [H32[g], gr["PC"], pbh], [Hbf[g]])
                O.stt("dve", H32[g][hs, 0:dk], H32[g][hs, 0:dk], gr["PC"][hs, c:c + 1], pbh[hs, 0:dk], ALU.mult, ALU.add, [H32[g], gr["PC"], pbh], [H32[g]])

        pbT = PS.get()
        for c in range(NCH):
            O.tr(pbT[:, c * 64:(c + 1) * 64], O32[1][:, c, :], identf[0:64, 0:64], [O32[1], identf], [pbT])
        OT = F[23]
        O.cp("act", OT[:, :], pbT[:, :BL], [pbT], [OT])
        pb1 = PS.get()
        O.mm(pb1[:, :BL], bonesf[:, :], OT[:, :], True, True, [bonesf, OT], [pb1])
        mean = F[24]
        O.ts("dve", mean[:, :], pb1[:, :BL], 1.0 / 64, None, ALU.mult, None, [pb1], [mean])
        sqo = F[25]
        O.tt("pool", sqo[:, :], OT[:, :], OT[:, :], ALU.mult, [OT], [sqo])
        pb2 = PS.get()
        O.mm(pb2[:, :BL], bonesf[:, :], sqo[:, :], True, True, [bonesf, sqo], [pb2])
        m2 = F[26]
        O.tt("pool", m2[:, :], mean[:, :], mean[:, :], ALU.mult, [mean], [m2])
        var = F[27]
        O.stt("dve", var[:, :], pb2[:, :BL], 1.0 / 64, m2[:, :], ALU.mult, ALU.subtract, [pb2, m2], [var])
        O.act(var[:, :], var[:, :], AF.Sqrt, [var], [var], bias=64e-5)
        S.op("dve", lambda g: g.reciprocal(var[:, :], var[:, :]), reads=[var], writes=[var])
        on = F[28]
        O.tt("pool", on[:, :], OT[:, :], mean[:, :], ALU.subtract, [OT, mean], [on])
        O.tt("pool", on[:, :], on[:, :], var[:, :], ALU.mult, [on, var], [on])
        O.ts("dve", on[:, :], on[:, :], pp[:, 23:24], pp[:, 24:25], ALU.mult, ALU.add, [on, pp], [on])
        O.tt("pool", on[:, :], on[:, :], bonus[:, :], ALU.add, [on, bonus], [on])
        O.tt("dve", ob[1][:, :], on[:, :], gate[:, :], ALU.mult, [on, gate], [ob[1]])
        S.dma("sp", odst[128:256, t0:t0 + BL], ob[1][:, :], reads=[ob[1]], writes=[D["agin_b"]])
        pbT = PS.get()
        for c in range(NCH):
            O.tr(pbT[:, c * 64:(c + 1) * 64], O32[0][:, c, :], identf[0:64, 0:64], [O32[0], identf], [pbT])
        OTg = F[23]
        O.cp("act", OTg[:, :], pbT[:, :BL], [pbT], [OTg])
        sqg = Bq[4]
        O.act(sqg[:, :], OTg[:, :], AF.Square, [OTg], [sqg])
        pb1 = PS.get()
        O.mm(pb1[:, :BL], onesb[:, :], sqg[:, :], True, True, [onesb, sqg], [pb1])
        rg = F[24]
        O.act(rg[:, :], pb1[:, :BL], AF.Sqrt, [pb1], [rg], bias=1e-6, scale=1.0 / 128)
        S.op("dve", lambda g: g.reciprocal(rg[:, :], rg[:, :]), reads=[rg], writes=[rg])
        O.tt("pool", rg[:, :], rg[:, :], OTg[:, :], ALU.mult, [rg, OTg], [rg])
        O.stt("dve", ob[0][:, :], rg[:, :], pp[:, 27:28], sz[:, :], ALU.mult, ALU.mult, [rg, pp, sz], [ob[0]])
        S.dma("sp", odst[0:128, t0:t0 + BL], ob[0][:, :], reads=[ob[0]], writes=[D["agin_b"]])


def panels(W):
    K, N = W.shape
    return np.ascontiguousarray(W.reshape(K // 128, 128, N // 128, 128).transpose(2, 1, 0, 3)).reshape(N // 128, 128, (K // 128) * 128)


def mixed_perm():
    perm = np.zeros(2048, np.int64)
    for c in range(8):
        perm[c * 256:c * 256 + 128] = c * 128 + np.arange(128)
        perm[c * 256 + 128:c * 256 + 256] = 1024 + c * 128 + np.arange(128)
    return perm


def consts():
    ident = np.eye(128, dtype=np.float32)
    bones = np.zeros((128, 128), np.float32)
    bones[:64, :64] = 1
    bones[64:, 64:] = 1
    s = np.arange(64)[:, None]
    t = np.arange(64)[None, :]
    masks = np.zeros((64, 3, 64), np.float32)
    masks[:, 0, :] = np.where(t >= s, 0.0, -30000.0)
    masks[:, 1, :] = (t > s)
    masks[:, 2, :] = (t >= s)
    return ident, bones, masks


def prep_shared(inp):
    f = lambda k: np.asarray(inp[k], np.float32)
    d = {}
    d["wout"] = panels(f("w_out")[0][mixed_perm(), :])
    d["wq"] = panels(f("xattn_wq")[0])
    d["wk"] = panels(f("xattn_wk")[0])
    d["wv"] = panels(f("xattn_wv")[0])
    d["wo"] = panels(f("xattn_wo")[0])
    d["wup"] = panels(f("mlp_w_up")[0])
    wd = f("mlp_w_down")[0]
    d["wdn"] = np.stack([panels(wd[q * 2048:(q + 1) * 2048]) for q in range(4)], 0)
    d["memT"] = np.ascontiguousarray(f("mem")[0].T)
    g = np.stack([f("norm_mix")[0], f("norm_xattn")[0], f("norm_mem")[0], f("norm_mlp")[0], f("norm_final")], -1)
    d["gains"] = np.ascontiguousarray(g.reshape(16, 128, 5).transpose(1, 0, 2))
    ident, bones, masks = consts()
    d["ident"] = ident
    d["bones"] = bones
    d["masks"] = masks
    return d


def core_cols(c):
    GW = 1024
    RW0 = 4 * GW + 16
    cols = []
    for j in range(4):
        cols += list(j * GW + c * 128 + np.arange(128))
    for j in range(3):
        cols += list(RW0 + j * 1024 + c * 128 + np.arange(128))
    cols += list(RW0 + 3072 + np.arange(64))
    cols += list(RW0 + 3072 + 64 + np.arange(64))
    cols += list(RW0 + 3072 + 128 + np.arange(160))
    cols += [4 * GW + c, 4 * GW + 8 + c]
    return np.array(cols)


def prep_core(inp, c):
    f = lambda k: np.asarray(inp[k], np.float32)
    d = {}
    cols = core_cols(c)
    W = f("w_in")[0][:, cols]
    d["win"] = np.ascontiguousarray(W.reshape(16, 128, 1186).transpose(1, 0, 2))
    pp = np.zeros((128, 32), np.float32)
    cw = f("gdn_conv_w")[0]
    for j in range(3):
        pp[:, j * 4:(j + 1) * 4] = cw[:, j * 1024 + c * 128:j * 1024 + (c + 1) * 128].T
    mu = f("rwkv_mu")[0]
    rs = slice(c * 128, (c + 1) * 128)
    pp[:, 12] = mu[0:1024][rs]
    pp[:, 13] = mu[1024:2048][rs]
    pp[:, 14] = mu[2048:3072][rs]
    pp[:64, 15] = mu[3072:3136]
    pp[64:, 15] = mu[3136:3200]
    pp[:, 16] = mu[3200:3328]
    pp[:32, 17] = mu[3328:3360]
    pp[:, 18] = f("rwkv_w0")[0][rs]
    pp[:, 19] = f("rwkv_a0")[0][rs]
    pp[:, 20] = f("rwkv_k_k")[0][rs]
    pp[:, 21] = f("rwkv_k_a")[0][rs]
    pp[:, 22] = f("rwkv_r_k")[0].reshape(-1)[rs]
    pp[:, 23] = f("rwkv_ln_w")[0][rs]
    pp[:, 24] = f("rwkv_ln_b")[0][rs]
    pp[:, 25] = f("gdn_A_log")[0][c]
    pp[:, 26] = f("gdn_dt_bias")[0][c]
    pp[:, 27] = f("gdn_norm_w")[0]
    d["pp"] = pp
    d["wa2"] = np.ascontiguousarray(np.concatenate([f("rwkv_w2")[0][:, rs], f("rwkv_a2")[0][:, rs]], 0))
    g2 = f("rwkv_g2")[0][:, rs]
    g2p = np.zeros((128, 2, 128), np.float32)
    g2p[:, 0] = g2[:128]
    g2p[:32, 1] = g2[128:]
    d["g2"] = g2p
    sel = np.zeros((128, 8), np.float32)
    sel[:, c] = 1
    d["sel"] = sel
    return d


def esel_const():
    e = np.zeros((34, 2, 128), np.float32)
    e[32, 0] = 1
    e[33, 1] = 1
    return e


def _common(nc, es, S, D, names):
    def sb(name, shape, dt):
        return Tile(es.enter_context(nc.sbuf_tensor("sc_" + name, shape, dt)))
    cst = {}
    cst["gains"] = sb("gains", [128, 16, 5], F32)
    S.dma("sp", cst["gains"][:, :, :], D["gains"], writes=[cst["gains"]])
    if "sel" in D:
        cst["sel"] = sb("sel", [128, 8], F32)
        S.dma("sp", cst["sel"][:, :], D["sel"], writes=[cst["sel"]])
    idf = sb("identf", [128, 128], F32)
    S.dma("sp", idf[:, :], D["ident"], writes=[idf])
    cst["identf"] = idf
    cst["identb"] = sb("identb", [128, 128], BF16)
    S.op("dve", lambda g: g.tensor_copy(cst["identb"][:, :], idf[:, :]), reads=[idf], writes=[cst["identb"]])
    cst["onesb"] = sb("onesb", [128, 128], BF16)
    S.op("pool", lambda g: g.memset(cst["onesb"][:, :], 1.0), writes=[cst["onesb"]])
    PS = Banks([Tile(es.enter_context(nc.psum_tensor(f"ps{i}", [128, 512], F32))) for i in range(6)])
    PSB = Banks([Tile(es.enter_context(nc.psum_tensor(f"psb{i}", [128, 1024], BF16))) for i in range(2)])
    return cst, PS, PSB


def _din_a(nc, D, SEQ):
    def din(name, shape, dt=F32):
        D[name] = nc.dram_tensor(name, shape, dt, kind="ExternalInput").ap()
    din("xT", [2048, SEQ]); din("win", [128, 16, 1186]); din("pp", [128, 32]); din("wa2", [128, 128]); din("g2", [128, 2, 128])
    din("bones", [128, 128]); din("masks", [64, 3, 64]); din("esel", [34, 2, 128])


def _din_b(nc, D, NT):
    def din(name, shape, dt=F32):
        D[name] = nc.dram_tensor(name, shape, dt, kind="ExternalInput").ap()
    din("xres", [2048, NT]); din("memT", [2048, 256])
    for k in ["wout", "wq", "wk", "wv", "wo"]:
        din(k, [16, 128, 2048])
    din("wup", [64, 128, 2048]); din("wdn", [4, 16, 128, 2048])
    D["outT"] = nc.dram_tensor("outT", [2048, NT], F32, kind="ExternalOutput").ap()


def build(SEQ):
    NT = SEQ // 8
    nc = bass.Bass("TRN2", target_bir_lowering=False)
    es = ExitStack()
    S = Sched(nc, es)
    D = {}
    _din_a(nc, D, SEQ)
    _din_b(nc, D, NT)
    for k, shp in [("gains", [128, 16, 5]), ("ident", [128, 128]), ("sel", [128, 8])]:
        D[k] = nc.dram_tensor(k, shp, F32, kind="ExternalInput").ap()
    agin = nc.dram_tensor("agin", [256, SEQ], BF16)
    agout = nc.dram_tensor("agout", [2048, SEQ], BF16)
    D["agout"] = agout.ap()
    D["agin_b"] = Buf()
    D["agout_b"] = Buf()
    cst, PS, PSB = _common(nc, es, S, D, None)
    esA = ExitStack()
    phase_a(nc, S, esA, D, SEQ, PS, PSB, cst, agin.ap())
    S.op("pool", lambda e: e.collective_compute("AllGather", ALU.bypass, replica_groups=[list(range(8))],
                                                ins=[agin.ap().opt()], outs=[agout.ap().opt()]),
         reads=[D["agin_b"]], writes=[D["agout_b"]])
    S.barrier()
    esA.close()
    toks = phase_b(nc, S, es, D, NT, PS, PSB, cst)
    for t in toks:
        S.wait_tok("sp", t)
    return nc


def build_a(SEQ):
    nc = bass.Bass("TRN2", target_bir_lowering=False)
    es = ExitStack()
    S = Sched(nc, es)
    D = {}
    _din_a(nc, D, SEQ)
    for k, shp in [("gains", [128, 16, 5]), ("ident", [128, 128])]:
        D[k] = nc.dram_tensor(k, shp, F32, kind="ExternalInput").ap()
    oT = nc.dram_tensor("oT", [256, SEQ], BF16, kind="ExternalOutput").ap()
    D["agin_b"] = Buf()
    cst, PS, PSB = _common(nc, es, S, D, None)
    phase_a(nc, S, es, D, SEQ, PS, PSB, cst, oT)
    S._deps("sp", [D["agin_b"]], [])
    return nc


def build_b(NT):
    nc = bass.Bass("TRN2", target_bir_lowering=False)
    es = ExitStack()
    S = Sched(nc, es)
    D = {}
    _din_b(nc, D, NT)
    for k, shp in [("gains", [128, 16, 5]), ("ident", [128, 128])]:
        D[k] = nc.dram_tensor(k, shp, F32, kind="ExternalInput").ap()
    D["mixT"] = nc.dram_tensor("mixT", [2048, NT], BF16, kind="ExternalInput").ap()
    D["agout_b"] = Buf()
    cst, PS, PSB = _common(nc, es, S, D, None)
    cst["sel"] = cst["identf"]
    toks = phase_b(nc, S, es, D, NT, PS, PSB, cst)
    for t in toks:
        S.wait_tok("sp", t)
    return nc


FUSED = False


def kernel(**inputs):
    inp = {k: np.asarray(v) for k, v in inputs.items()}
    x = np.asarray(inp["x"], np.float32)
    SEQ = x.shape[1]
    NT = SEQ // 8
    xT = np.ascontiguousarray(x[0].T)
    sh = prep_shared(inp)
    es_c = esel_const()
    cores = [prep_core(inp, c) for c in range(8)]
    if FUSED:
        nc = build(SEQ)
        in_maps = []
        for c in range(8):
            pc = cores[c]
            m = {k: sh[k] for k in ["wout", "wq", "wk", "wv", "wo", "wup", "wdn", "memT", "gains", "ident", "bones", "masks"]}
            m["xT"] = xT
            m["xres"] = np.ascontiguousarray(xT[:, c * NT:(c + 1) * NT])
            m["esel"] = es_c
            for k in ["win", "pp", "wa2", "g2", "sel"]:
                m[k] = pc[k]
            in_maps.append(m)
        res = run_bass_kernel_spmd(nc, in_maps, core_ids=list(range(8)))
    else:
        nca = build_a(SEQ)
        in_a = []
        for c in range(8):
            pc = cores[c]
            m = {k: sh[k] for k in ["gains", "ident", "bones", "masks"]}
            m["xT"] = xT
            m["esel"] = es_c
            for k in ["win", "pp", "wa2", "g2"]:
                m[k] = pc[k]
            in_a.append(m)
        ra = run_bass_kernel_spmd(nca, in_a, core_ids=list(range(8)))
        mixT = np.concatenate([np.asarray(ra.results[c]["oT"]) for c in range(8)], axis=0)
        ncb = build_b(NT)
        in_b = []
        for c in range(8):
            m = {k: sh[k] for k in ["wout", "wq", "wk", "wv", "wo", "wup", "wdn", "memT", "gains", "ident"]}
            m["xres"] = np.ascontiguousarray(xT[:, c * NT:(c + 1) * NT])
            m["mixT"] = np.ascontiguousarray(mixT[:, c * NT:(c + 1) * NT])
            in_b.append(m)
        res = run_bass_kernel_spmd(ncb, in_b, core_ids=list(range(8)))
    outT = np.concatenate([np.asarray(res.results[c]["outT"], np.float32) for c in range(8)], axis=1)
    return np.ascontiguousarray(outT.T)[None, :, :].astype(np.float32)
```
